# Optimizing a Trainium2 kernel written in Bass

```python
import math
import numpy as np
import jax
import jax.numpy as jnp
from jax import lax

D_MODEL = 2048
BATCH = 4
SEQ = 2048
DEPTH = 2

GRID_W = 64
CTX_LEN = 256
NORM_EPS = 1e-6
N_MOD = 9
FFN_HIDDEN = 5632
GDN_HEADS = 8
GDN_DK = 128
GDN_DV = 128
GDN_CONV = 5
GDN_CHUNK = 64
ATT_HEADS = 8
ATT_KV_HEADS = 2
ATT_HEAD_DIM = 128
ROPE_THETA = 10000.0
Q_BLOCK = 128
FOURIER_GROUPS = 4
W_QA = GDN_HEADS * GDN_DK
W_VA = GDN_HEADS * GDN_DV
W_QB = ATT_HEADS * ATT_HEAD_DIM
W_KB = ATT_KV_HEADS * ATT_HEAD_DIM
SPLITS = (W_QA, W_QA, W_VA, W_VA, 4 * GDN_HEADS, W_QB, W_KB, W_KB)
IN_WIDTH = sum(SPLITS)
MIX_WIDTH = W_VA + W_QB

kernel_name = 'hybrid_gdn_gqa_fourier_dit'


def rmsnorm(x, gain):
    xf = x.astype(jnp.float32)
    y = xf * lax.rsqrt(jnp.mean(xf * xf, axis=-1, keepdims=True) + NORM_EPS)
    return (y * gain.astype(jnp.float32)).astype(x.dtype)


def head_rms(x, gain):
    xf = x.astype(jnp.float32)
    return xf * lax.rsqrt(jnp.mean(xf * xf, axis=-1, keepdims=True) + NORM_EPS) * gain.astype(jnp.float32)


def l2norm(x):
    return x * lax.rsqrt(jnp.sum(x * x, axis=-1, keepdims=True) + NORM_EPS)


def modnorm(x, gain, shift, scale):
    return rmsnorm(x, gain) * (1.0 + scale) + shift


def swiglu(h, w_gu, w_down):
    gate, up = jnp.split(h @ w_gu, 2, axis=-1)
    return (jax.nn.silu(gate) * up) @ w_down


def ffn_sublayer(h, m, base, gain, w_gu, w_down):
    u = modnorm(h, gain, m[:, base], m[:, base + 1])
    return h + 0.5 * m[:, base + 2] * swiglu(u, w_gu, w_down)


def short_conv(x, w):
    y = lax.conv_general_dilated(x, w[:, None, :].astype(x.dtype), window_strides=(1,),
                                 padding=[(GDN_CONV // 2, GDN_CONV // 2)],
                                 dimension_numbers=('NWC', 'WIO', 'NWC'),
                                 feature_group_count=x.shape[-1])
    return jax.nn.silu(y)


def to_heads(x, n_heads):
    b, t, _ = x.shape
    return x.reshape(b, t, n_heads, -1).transpose(0, 2, 1, 3).astype(jnp.float32)


def chunk_gated_delta(q, k, v, g, beta, s0):
    b, h, t, dk = q.shape
    cs = GDN_CHUNK
    n = t // cs
    q = (q * dk ** -0.5).reshape(b, h, n, cs, dk)
    k = k.reshape(b, h, n, cs, dk)
    v = v.reshape(b, h, n, cs, -1)
    beta = beta.reshape(b, h, n, cs)
    g = jnp.cumsum(g.reshape(b, h, n, cs), axis=-1)
    incl = jnp.tril(jnp.ones((cs, cs), dtype=bool))
    strict = jnp.tril(jnp.ones((cs, cs), dtype=bool), -1)
    decay = jnp.exp(jnp.where(incl, g[..., :, None] - g[..., None, :], -jnp.inf))
    kk = jnp.einsum('bhnid,bhnjd->bhnij', k, k)
    lower = jnp.where(strict, beta[..., :, None] * kk * decay, 0.0)
    eye = jnp.eye(cs, dtype=jnp.float32)
    tmat = lax.linalg.triangular_solve(eye + lower, jnp.broadcast_to(eye, lower.shape),
                                       left_side=True, lower=True, unit_diagonal=True)
    u = jnp.einsum('bhnij,bhnjd->bhnid', tmat, v * beta[..., None])
    w = jnp.einsum('bhnij,bhnjd->bhnid', tmat, k * (beta * jnp.exp(g))[..., None])
    qk = jnp.einsum('bhnid,bhnjd->bhnij', q, k) * decay
    q_dec = q * jnp.exp(g)[..., None]
    k_dec = k * jnp.exp(g[..., -1:] - g)[..., None]
    g_tot = jnp.exp(g[..., -1])

    def step(s, inp):
        qd, kd, u_c, w_c, a_c, gt = inp
        v_new = u_c - jnp.einsum('bhcd,bhde->bhce', w_c, s)
        o = jnp.einsum('bhcd,bhde->bhce', qd, s) + jnp.einsum('bhij,bhje->bhie', a_c, v_new)
        s = s * gt[..., None, None] + jnp.einsum('bhcd,bhce->bhde', kd, v_new)
        return s, o

    xs = tuple(jnp.moveaxis(a, 2, 0) for a in (q_dec, k_dec, u, w, qk, g_tot))
    s_final, o = lax.scan(step, s0, xs)
    return jnp.moveaxis(o, 0, 2).reshape(b, h, t, -1), s_final


def gdn_inputs(qa, ka, va, gates, conv_w, a_log, dt_bias):
    qkv = short_conv(jnp.concatenate([qa, ka, va], axis=-1), conv_w)
    q, k, v = jnp.split(qkv, [W_QA, 2 * W_QA], axis=-1)
    q, k, v = l2norm(to_heads(q, GDN_HEADS)), l2norm(to_heads(k, GDN_HEADS)), to_heads(v, GDN_HEADS)
    b, t, _ = gates.shape
    gates = gates.astype(jnp.float32).reshape(b, t, 4, GDN_HEADS).transpose(2, 0, 3, 1)
    dt = jax.nn.softplus(gates[:2] + dt_bias.astype(jnp.float32)[:, None, :, None])
    g = -jnp.exp(a_log.astype(jnp.float32))[:, None, :, None] * dt
    beta = jax.nn.sigmoid(gates[2:])
    return q, k, v, g, beta


def gdn_bidir(q, k, v, g, beta, s_fwd, s_bwd):
    o_f, s_f = chunk_gated_delta(q, k, v, g[0], beta[0], s_fwd)
    flip = lambda a: jnp.flip(a, axis=2)
    o_b, s_b = chunk_gated_delta(flip(q), flip(k), flip(v), flip(g[1]), flip(beta[1]), s_bwd)
    return o_f + flip(o_b), s_f, s_b


def gdn_output(o, z, gain):
    b, h, t, dv = o.shape
    o = head_rms(o.transpose(0, 2, 1, 3), gain)
    return (o * jax.nn.silu(z.astype(jnp.float32).reshape(b, t, h, dv))).reshape(b, t, h * dv)


def rope_axis(x, pos):
    half = x.shape[-1] // 2
    inv = ROPE_THETA ** (-jnp.arange(half, dtype=jnp.float32) / half)
    ang = pos.astype(jnp.float32)[:, None] * inv
    ang = ang.reshape((ang.shape[0],) + (1,) * (x.ndim - 3) + (half,))
    cos, sin = jnp.cos(ang), jnp.sin(ang)
    x1, x2 = x[..., :half], x[..., half:]
    return jnp.concatenate([x1 * cos - x2 * sin, x1 * sin + x2 * cos], axis=-1)


def rope_2d(x, row, col):
    half = x.shape[-1] // 2
    return jnp.concatenate([rope_axis(x[..., :half], row), rope_axis(x[..., half:], col)], axis=-1)


def gqa_qkv(qb, kb, vb, q_gain, k_gain):
    b, t, _ = qb.shape
    q = head_rms(qb.reshape(b, t, ATT_KV_HEADS, ATT_HEADS // ATT_KV_HEADS, ATT_HEAD_DIM), q_gain)
    k = head_rms(kb.reshape(b, t, ATT_KV_HEADS, ATT_HEAD_DIM), k_gain)
    v = vb.reshape(b, t, ATT_KV_HEADS, ATT_HEAD_DIM).astype(jnp.float32)
    return q, k, v


def attend(q, k, v):
    s = jnp.einsum('bqkgd,bskd->bkgqs', q, k) * (ATT_HEAD_DIM ** -0.5)
    p = jax.nn.softmax(s, axis=-1)
    return jnp.einsum('bkgqs,bskd->bqkgd', p, v)


def hybrid_mixer(u_lat, u_ctx, w_in, conv_w, a_log, dt_bias, gdn_gain, q_gain, k_gain, w_out, row, col, ctx_out):
    offs = np.cumsum(SPLITS)[:-1].tolist()
    pl = jnp.split(u_lat @ w_in, offs, axis=-1)
    pc = jnp.split(u_ctx @ w_in, offs, axis=-1)
    qc, kc, vc, gc, bc = gdn_inputs(pc[0], pc[1], pc[2], pc[4], conv_w, a_log, dt_bias)
    s0 = jnp.zeros(qc.shape[:2] + (GDN_DK, GDN_DV), jnp.float32)
    oc_a, s_fwd, s_bwd = gdn_bidir(qc, kc, vc, gc, bc, s0, s0)
    ql, kl, vl, gl, bl = gdn_inputs(pl[0], pl[1], pl[2], pl[4], conv_w, a_log, dt_bias)
    ol_a, _, _ = gdn_bidir(ql, kl, vl, gl, bl, s_fwd, s_bwd)
    qcb, kcb, vcb = gqa_qkv(pc[5], pc[6], pc[7], q_gain, k_gain)
    qlb, klb, vlb = gqa_qkv(pl[5], pl[6], pl[7], q_gain, k_gain)
    qlb, klb = rope_2d(qlb, row, col), rope_2d(klb, row, col)
    k_all = jnp.concatenate([kcb, klb], axis=1)
    v_all = jnp.concatenate([vcb, vlb], axis=1)
    b, t = qlb.shape[:2]
    q_blocks = jnp.moveaxis(qlb.reshape((b, t // Q_BLOCK, Q_BLOCK) + qlb.shape[2:]), 1, 0)
    ol_b = lax.map(lambda qb: attend(qb, k_all, v_all), q_blocks)
    ol_b = jnp.moveaxis(ol_b, 0, 1).reshape(b, t, W_QB)
    y_lat = jnp.concatenate([gdn_output(ol_a, pl[3], gdn_gain), ol_b], axis=-1).astype(u_lat.dtype) @ w_out
    if not ctx_out:
        return y_lat, None
    oc_b = attend(qcb, kcb, vcb).reshape(b, -1, W_QB)
    y_ctx = jnp.concatenate([gdn_output(oc_a, pc[3], gdn_gain), oc_b], axis=-1).astype(u_ctx.dtype) @ w_out
    return y_lat, y_ctx


def fourier_mix(u, w_f):
    b, t, d = u.shape
    ug = u.astype(jnp.float32).reshape(b, t, FOURIER_GROUPS, d // FOURIER_GROUPS)
    y = jnp.fft.fft2(ug, axes=(1, 3), norm='ortho').real
    return y.reshape(b, t, d).astype(u.dtype) @ w_f


def setup_inputs(seed: int = 0) -> dict:
    key = jax.random.key(seed)
    ks = jax.random.split(key, 20)
    f32 = jnp.float32
    n_even = (DEPTH + 1) // 2
    n_odd = DEPTH // 2

    def normal(k, shape, scale):
        return jax.random.normal(k, shape, f32) * scale

    def gain(k, shape):
        return 1.0 + 0.02 * jax.random.normal(k, shape, f32)

    dt = jnp.exp(jax.random.uniform(ks[10], (n_even, 2, GDN_HEADS), f32, minval=math.log(1e-3), maxval=math.log(1e-1)))
    return {
        'x': normal(ks[0], (BATCH, SEQ, D_MODEL), 1.0),
        'c': normal(ks[1], (BATCH, D_MODEL), 1.0),
        'ctx': normal(ks[2], (BATCH, CTX_LEN, D_MODEL), 1.0),
        'c_ctx': normal(ks[3], (D_MODEL,), 1.0),
        'w_mod': normal(ks[4], (DEPTH, D_MODEL, N_MOD * D_MODEL), 0.5 * D_MODEL ** -0.5),
        'b_mod': normal(ks[5], (DEPTH, N_MOD * D_MODEL), 0.02),
        'norm_w': gain(ks[6], (DEPTH, 3, D_MODEL)),
        'ffn_w_gu': normal(ks[7], (DEPTH, 2, D_MODEL, 2 * FFN_HIDDEN), D_MODEL ** -0.5),
        'ffn_w_down': normal(ks[8], (DEPTH, 2, FFN_HIDDEN, D_MODEL), FFN_HIDDEN ** -0.5),
        'w_in': normal(ks[9], (n_even, D_MODEL, IN_WIDTH), D_MODEL ** -0.5),
        'conv_w': normal(ks[11], (n_even, GDN_CONV, 2 * W_QA + W_VA), GDN_CONV ** -0.5),
        'a_log': jnp.log(jax.random.uniform(ks[12], (n_even, 2, GDN_HEADS), f32, minval=1.0, maxval=16.0)),
        'dt_bias': dt + jnp.log(-jnp.expm1(-dt)),
        'gdn_norm': gain(ks[13], (n_even, GDN_DV)),
        'q_norm': gain(ks[14], (n_even, ATT_HEAD_DIM)),
        'k_norm': gain(ks[15], (n_even, ATT_HEAD_DIM)),
        'w_out': normal(ks[16], (n_even, MIX_WIDTH, D_MODEL), MIX_WIDTH ** -0.5),
        'w_fourier': normal(ks[17], (n_odd, D_MODEL, D_MODEL), D_MODEL ** -0.5),
        'final_norm': gain(ks[18], (D_MODEL,)),
    }


def reference(x, c, ctx, c_ctx, w_mod, b_mod, norm_w, ffn_w_gu, ffn_w_down, w_in, conv_w, a_log, dt_bias,
              gdn_norm, q_norm, k_norm, w_out, w_fourier, final_norm):
    b, t, d = x.shape
    rows = t // GRID_W
    row = jnp.repeat(jnp.arange(rows), GRID_W)
    col = jnp.tile(jnp.arange(GRID_W), rows)
    silu_c = jax.nn.silu(c)
    silu_cc = jax.nn.silu(c_ctx)[None, :]
    h_lat, h_ctx = x, ctx
    for layer in range(DEPTH):
        ctx_out = any(j % 2 == 0 for j in range(layer + 1, DEPTH))
        ctx_in = layer % 2 == 0 or ctx_out
        m_lat = (silu_c @ w_mod[layer] + b_mod[layer]).reshape(b, N_MOD, 1, d)
        m_ctx = (silu_cc @ w_mod[layer] + b_mod[layer]).reshape(1, N_MOD, 1, d)
        h_lat = ffn_sublayer(h_lat, m_lat, 0, norm_w[layer, 0], ffn_w_gu[layer, 0], ffn_w_down[layer, 0])
        if ctx_in:
            h_ctx = ffn_sublayer(h_ctx, m_ctx, 0, norm_w[layer, 0], ffn_w_gu[layer, 0], ffn_w_down[layer, 0])
        u_lat = modnorm(h_lat, norm_w[layer, 1], m_lat[:, 3], m_lat[:, 4])
        if layer % 2 == 0:
            e = layer // 2
            u_ctx = modnorm(h_ctx, norm_w[layer, 1], m_ctx[:, 3], m_ctx[:, 4])
            y_lat, y_ctx = hybrid_mixer(u_lat, u_ctx, w_in[e], conv_w[e], a_log[e], dt_bias[e], gdn_norm[e],
                                        q_norm[e], k_norm[e], w_out[e], row, col, ctx_out)
        else:
            o = layer // 2
            y_lat = fourier_mix(u_lat, w_fourier[o])
            if ctx_out:
                y_ctx = fourier_mix(modnorm(h_ctx, norm_w[layer, 1], m_ctx[:, 3], m_ctx[:, 4]), w_fourier[o])
        h_lat = h_lat + m_lat[:, 5] * y_lat
        h_lat = ffn_sublayer(h_lat, m_lat, 6, norm_w[layer, 2], ffn_w_gu[layer, 1], ffn_w_down[layer, 1])
        if ctx_out:
            h_ctx = h_ctx + m_ctx[:, 5] * y_ctx
            h_ctx = ffn_sublayer(h_ctx, m_ctx, 6, norm_w[layer, 2], ffn_w_gu[layer, 1], ffn_w_down[layer, 1])
    return rmsnorm(h_lat, final_norm)
```

```python
import numpy as np
import ml_dtypes
from contextlib import ExitStack
import concourse.bass as bass
import concourse.mybir as mybir
from concourse.bass_utils import run_bass_kernel_spmd

F32 = mybir.dt.float32
BF16 = mybir.dt.bfloat16
AF = mybir.ActivationFunctionType
ALU = mybir.AluOpType
AX = mybir.AxisListType
NPBF = ml_dtypes.bfloat16

NCORES = 8
D = 2048
KC = 16
FH = 5632
NHC = 44
EPS = 1e-6


class Buf:
    __slots__ = ("name", "lw", "rd", "x")

    def __init__(self, name="", x=False):
        self.name = name
        self.lw = None
        self.rd = []
        self.x = x


class Prog:
    ENGS = ("pe", "act", "dve", "pool", "sp")
    NDMA = 20
    CMAX = 800

    def __init__(self, nc):
        self.nc = nc
        self.ops = []

    def op(self, eng, fn, r=(), w=(), dma=False):
        self.ops.append([eng, fn, tuple(r), tuple(w), dma or eng == "sp"])

    def mm(self, out, lhsT, rhs, start, stop, r=(), w=()):
        self.op("pe", lambda e: e.matmul(out, lhsT, rhs, start=start, stop=stop), r, w)

    def tr(self, out, in_, ident, r=(), w=()):
        self.op("pe", lambda e: e.transpose(out, in_, ident), r, w)

    def act(self, out, in_, func, r=(), w=(), **kw):
        self.op("act", lambda e: e.activation(out, in_, func, **kw), r, w)

    def dma(self, eng, out, in_, r=(), w=()):
        self.op(eng, lambda e: e.dma_start(out=out, in_=in_), r, w, dma=True)

    def emit(self):
        nc = self.nc
        ops = self.ops
        n = len(ops)
        deps = [None] * n
        needed = [False] * n
        for i, (eng, fn, r, w, dma) in enumerate(ops):
            d = set()
            xr = tuple(b for b in r if b.x)
            if xr:
                r = tuple(b for b in r if not b.x)
                w = tuple(w) + xr
            for b in r:
                if b.lw is not None:
                    d.add(b.lw)
            for b in w:
                if b.lw is not None:
                    d.add(b.lw)
                d.update(b.rd)
            for b in r:
                b.rd.append(i)
            for b in w:
                b.lw = i
                b.rd = []
            d.discard(i)
            if eng == "pe":
                d = {j for j in d if not (ops[j][0] == "pe" and not ops[j][4])}
            deps[i] = d
            for j in d:
                needed[j] = True
        sig = [None] * n
        cnt = {e: 0 for e in self.ENGS}
        dcnt = {e: 0 for e in self.ENGS}
        dma_prev = {}
        pre_wait = [None] * n
        last_dma = {}
        for i, (eng, fn, r, w, dma) in enumerate(ops):
            if dma:
                k = dcnt[eng] % self.NDMA
                dcnt[eng] += 1
                key = ("d", eng, k)
                prev = dma_prev.get(key, 0)
                if prev:
                    pre_wait[i] = (key, prev)
                dma_prev[key] = prev + 16
                sig[i] = (key, prev + 16)
                last_dma[key] = prev + 16
            elif needed[i]:
                cnt[eng] += 1
                c = cnt[eng]
                sig[i] = (("c", eng, (c - 1) // self.CMAX), (c - 1) % self.CMAX + 1)
        keys = []
        for s in sig:
            if s is not None and s[0] not in keys:
                keys.append(s[0])
        with ExitStack() as es:
            sems = {}
            for k in keys:
                sems[k] = es.enter_context(nc.semaphore("s_" + "_".join(str(x) for x in k)))
            block = es.enter_context(nc.Block())
            per_eng = {e: [i for i in range(n) if ops[i][0] == e] for e in self.ENGS}

            def run_engine(e, ename):
                seen = {}
                for i in per_eng[ename]:
                    eng, fn, r, w, dma = ops[i]
                    waits = {}
                    if pre_wait[i] is not None:
                        waits[pre_wait[i][0]] = pre_wait[i][1]
                    for j in deps[i]:
                        k, v = sig[j]
                        if waits.get(k, 0) < v:
                            waits[k] = v
                    for k, v in waits.items():
                        if seen.get(k, 0) < v:
                            e.wait_ge(sems[k], v)
                            seen[k] = v
                    ins = fn(e)
                    if sig[i] is not None:
                        k, v = sig[i]
                        ins.then_inc(sems[k], 16 if dma else 1)
                if ename == "sp":
                    for k, v in last_dma.items():
                        if seen.get(k, 0) < v:
                            e.wait_ge(sems[k], v)

            @block.tensor
            def _(e):
                run_engine(e, "pe")

            @block.scalar
            def _(e):
                run_engine(e, "act")

            @block.vector
            def _(e):
                run_engine(e, "dve")

            @block.gpsimd
            def _(e):
                run_engine(e, "pool")

            @block.sync
            def _(e):
                run_engine(e, "sp")


def _run(nc, in_maps):
    res = run_bass_kernel_spmd(nc, in_maps, core_ids=list(range(NCORES)))
    return res.results


MODC = 2304
MODB = 256


def build_mod():
    nc = bass.Bass("TRN2", target_bir_lowering=False)
    nb = MODC // MODB
    cT = nc.dram_tensor("cT", [128, KC, 5], F32, kind="ExternalInput").ap()
    wt = nc.dram_tensor("wt", [2 * nb, 128, KC, MODB], F32, kind="ExternalInput").ap()
    bm = nc.dram_tensor("bm", [2, MODC], F32, kind="ExternalInput").ap()
    mo = nc.dram_tensor("mo", [2, 5, MODC], F32, kind="ExternalOutput").ap()
    P = Prog(nc)
    with ExitStack() as es:
        sb = lambda name, shape, dt: es.enter_context(nc.sbuf_tensor(name, shape, dt))
        ct = sb("ct", [128, KC, 5], F32)
        sg = sb("sg", [128, KC, 5], F32)
        st = sb("st", [128, KC, 5], F32)
        wts = [sb(f"w{i}", [128, KC, MODB], F32) for i in range(3)]
        brep = sb("brep", [5, 2, MODC], F32)
        res = sb("res", [5, 2, MODC], F32)
        pss = [es.enter_context(nc.psum_tensor(f"ps{i}", [128, 512], F32)) for i in range(2)]
        b_ct, b_st, b_sg, b_brep, b_res = Buf(), Buf(), Buf(), Buf(), Buf()
        b_w = [Buf() for _ in wts]
        b_ps = [Buf(x=True) for _ in pss]
        P.dma("sp", ct[:], cT, w=[b_ct])
        for l in range(2):
            P.dma("sp", brep[:, l, :], bm[l:l + 1, :].partition_broadcast(5), w=[b_brep])
        P.act(sg[:], ct[:], AF.Sigmoid, r=[b_ct], w=[b_sg])
        P.op("dve", lambda e: e.tensor_tensor(st[:], ct[:], sg[:], ALU.mult), r=[b_ct, b_sg], w=[b_st])
        for t in range(2 * nb):
            l, j = divmod(t, nb)
            wi = t % 3
            P.dma("sp", wts[wi][:], wt[t], w=[b_w[wi]])
            ps = pss[t % 2]
            for kc in range(KC):
                P.mm(ps[0:5, 0:MODB], st[:, kc, :], wts[wi][:, kc, :], kc == 0, kc == KC - 1,
                     r=[b_st, b_w[wi]], w=[b_ps[t % 2]])
            P.op("dve", (lambda ps=ps, l=l, j=j: lambda e: e.tensor_tensor(
                res[:, l, j * MODB:(j + 1) * MODB], ps[0:5, 0:MODB], brep[:, l, j * MODB:(j + 1) * MODB], ALU.add))(),
                r=[b_ps[t % 2], b_brep], w=[b_res])
        P.dma("sp", mo.rearrange("l r c -> r l c"), res[:], r=[b_res], w=[Buf()])
        P.emit()
    return nc


def run_mod(c, c_ctx, w_mod, b_mod):
    nb = MODC // MODB
    cin = np.concatenate([c, c_ctx[None, :]], axis=0)
    cT = np.ascontiguousarray(cin.reshape(5, KC, 128).transpose(2, 1, 0))
    in_maps = []
    for cid in range(NCORES):
        ws = w_mod[:, :, cid * MODC:(cid + 1) * MODC]
        wt = ws.reshape(2, KC, 128, nb, MODB).transpose(0, 3, 2, 1, 4).reshape(2 * nb, 128, KC, MODB)
        in_maps.append({"cT": cT, "wt": np.ascontiguousarray(wt),
                        "bm": np.ascontiguousarray(b_mod[:, cid * MODC:(cid + 1) * MODC])})
    nc = build_mod()
    res = _run(nc, in_maps)
    m = np.concatenate([r["mo"] for r in res], axis=2)
    return m.reshape(2, 5, 9, D)


NG = 8
NWD = 4
HBC = 2


def build_stage(nt, proj, nffn, post, ctx_tile):
    nc = bass.Bass("TRN2", target_bir_lowering=False)
    NTOK = nt * 128
    ncls = 2 if ctx_tile else 1
    nmod = nffn + (1 if post == "modnorm" else 0)
    NV = 3 * nmod * ncls
    NR = (1 if proj else 0) + nffn * ncls + (1 if post == "final" else 0)
    din = lambda name, shape, dt: nc.dram_tensor(name, shape, dt, kind="ExternalInput").ap()
    dout = lambda name, shape, dt: nc.dram_tensor(name, shape, dt, kind="ExternalOutput").ap()
    hin = din("hin", [NTOK, D], F32)
    vT = din("vT", [128, NV, KC], F32)
    vrow = din("vrow", [NR, D], F32)
    ident_d = din("ident", [128, 128], F32)
    wgu_d = [din(f"wgu{f}", [2 * NHC, 128, KC, 128], F32) for f in range(nffn)]
    wd_d = [din(f"wd{f}", [NHC, 128, D], F32) for f in range(nffn)]
    if proj:
        aT_d = din("aT", [128, KC, 1024], BF16)
        wp_d = din("wp", [4, 128, KC, 512], F32)
    if post == "modnorm":
        uT_o = dout("uT_o", [128, KC, NTOK], BF16)
        h_o = dout("h_o", [1024, D], F32)
    else:
        y_o = dout("y_o", [1024, D], F32)
    groups = []
    o = 0
    while o < NTOK:
        sz = min(512, NTOK - o)
        groups.append((o, sz))
        o += sz
    P = Prog(nc)
    with ExitStack() as es:
        sb = lambda name, shape, dt: es.enter_context(nc.sbuf_tensor(name, shape, dt))
        h = sb("h", [128, nt, D], F32)
        uT = sb("uT", [128, KC, NTOK], BF16)
        ring = sb("ring", [128, NG * KC * 128], BF16)
        wdr = sb("wdr", [128, NWD, D], BF16)
        hT = sb("hT", [128, 2, HBC, NTOK], BF16)
        grep = sb("grep", [128, 2, D], F32)
        hn = sb("hn", [128, D], F32)
        tmp = sb("tmp", [128, 2, 512], F32)
        sgt = sb("sgt", [128, 2, 512], BF16)
        vt = sb("vt", [128, NV, KC], F32)
        ab = sb("ab", [128, nmod * ncls, KC], F32)
        ident = sb("identt", [128, 128], F32)
        ss = sb("ss", [128, 4 * nt * (nmod + 1)], F32)
        ps = [es.enter_context(nc.psum_tensor(f"ps{i}", [128, 512], F32)) for i in range(8)]
        b_h = [Buf(f"h{t}") for t in range(nt)]
        b_uT = [Buf() for _ in groups]
        b_ring = [Buf() for _ in range(NG)]
        b_wd = [Buf() for _ in range(NWD)]
        b_hT = [[Buf() for _ in groups] for _ in range(2)]
        b_grep = [Buf(), Buf()]
        b_hn, b_vt, b_ab, b_id = Buf(), Buf(), Buf(), Buf()
        b_tmp = [Buf(), Buf()]
        b_sgt = [Buf(), Buf()]
        b_ps = [Buf(x=True) for _ in range(8)]
        b_ss = Buf()
        gslot = lambda s: ring[:, s * 2048:(s + 1) * 2048].rearrange("p (k n) -> p k n", k=KC)
        pslot = lambda i: ring[:, i * 8192:(i + 1) * 8192].rearrange("p (k n) -> p k n", k=KC)
        grp_of_tile = lambda t: [gi for gi, (o, sz) in enumerate(groups) if o <= t * 128 < o + sz][0]
        cls = lambda t: 1 if (ctx_tile and t == nt - 1) else 0

        P.dma("sp", ident[:], ident_d, w=[b_id])
        P.dma("sp", vt[:], vT, w=[b_vt])
        for t in range(nt):
            P.dma("sp", h[:, t, :], hin[t * 128:(t + 1) * 128, :], w=[b_h[t]])
        for mi in range(nmod * ncls):
            P.op("dve", (lambda mi=mi: lambda e: e.scalar_tensor_tensor(
                ab[:, mi, :], vt[:, 3 * mi + 2, :], 1.0, vt[:, 3 * mi + 0, :], ALU.add, ALU.mult))(),
                r=[b_vt], w=[b_ab])
        rowi = [0]
        sscol = [0]

        def load_gate(slot, scale):
            ri = rowi[0]
            rowi[0] += 1
            P.dma("sp", grep[:, slot, :], vrow[ri:ri + 1, :].partition_broadcast(128), w=[b_grep[slot]])
            if scale != 1.0:
                P.op("dve", lambda e: e.tensor_scalar(grep[:, slot, :], grep[:, slot, :], scale, None, ALU.mult),
                     r=[b_grep[slot]], w=[b_grep[slot]])

        evq = [0]

        def rstd_of(t):
            c0 = sscol[0]
            sscol[0] += 2
            P.act(hn[:], h[:, t, :], AF.Square, r=[b_h[t]], w=[b_hn, b_ss], accum_out=ss[:, c0:c0 + 1])
            P.act(ss[:, c0 + 1:c0 + 2], ss[:, c0:c0 + 1], AF.Sqrt, r=[b_ss], w=[b_ss], scale=1.0 / D, bias=EPS)
            P.op("dve", lambda e: e.reciprocal(ss[:, c0:c0 + 1], ss[:, c0 + 1:c0 + 2]), r=[b_ss], w=[b_ss])
            return c0

        def modnorm_to_uT(mi_base):
            for t in range(nt):
                c0 = rstd_of(t)
                mi = mi_base * ncls + cls(t)
                P.act(hn[:], h[:, t, :], AF.Copy, r=[b_h[t], b_ss], w=[b_hn], scale=ss[:, c0:c0 + 1])
                gi = grp_of_tile(t)
                for q in range(4):
                    pb = 6 + (evq[0] % 2)
                    evq[0] += 1
                    for kk in range(4):
                        kc = q * 4 + kk
                        P.tr(ps[pb][:, kk * 128:(kk + 1) * 128], hn[:, kc * 128:(kc + 1) * 128], ident[:],
                             r=[b_hn, b_id], w=[b_ps[pb]])
                    for kk in range(4):
                        kc = q * 4 + kk
                        dst = uT[:, kc, t * 128:(t + 1) * 128]
                        src = ps[pb][:, kk * 128:(kk + 1) * 128]
                        if kk % 2 == 0:
                            P.op("dve", (lambda dst=dst, src=src, mi=mi, kc=kc: lambda e: e.tensor_scalar(
                                dst, src, ab[:, mi, kc:kc + 1], vt[:, 3 * mi + 1, kc:kc + 1], ALU.mult, ALU.add))(),
                                r=[b_ps[pb], b_ab, b_vt], w=[b_uT[gi]])
                        else:
                            P.act(dst, src, AF.Identity, r=[b_ps[pb], b_ab, b_vt], w=[b_uT[gi]],
                                  scale=ab[:, mi, kc:kc + 1], bias=vt[:, 3 * mi + 1, kc:kc + 1])

        yq = [0]

        def accumulate(t, cb, pb, gsl):
            k = yq[0] % 2
            yq[0] += 1
            cs = slice(cb * 512, (cb + 1) * 512)
            P.op("dve", lambda e: e.tensor_tensor(tmp[:, k, :], ps[pb][:, :], grep[:, gsl, cs], ALU.mult),
                 r=[b_ps[pb], b_grep[gsl]], w=[b_tmp[k]])
            P.op("pool", lambda e: e.tensor_tensor(h[:, t, cs], h[:, t, cs], tmp[:, k, :], ALU.add),
                 r=[b_tmp[k], b_h[t]], w=[b_h[t]])

        if proj:
            load_gate(0, 1.0)
            P.dma("sp", uT[:, :, 0:1024], aT_d, w=b_uT)
            for cb in range(4):
                sl = cb % 2
                P.dma("pool", pslot(sl), wp_d[cb], w=b_ring[4 * sl:4 * sl + 4])
                for t in range(nt):
                    pb = 4 + (yq[0] % 2)
                    for kc in range(KC):
                        P.mm(ps[pb][:, :], uT[:, kc, t * 128:(t + 1) * 128], pslot(sl)[:, kc, :], kc == 0, kc == KC - 1,
                             r=[b_uT[grp_of_tile(t)]] + b_ring[4 * sl:4 * sl + 4], w=[b_ps[pb]])
                    accumulate(t, cb, pb, 0)

        for f in range(nffn):
            gbase = 1 if (proj and f == 0) else 0
            if ncls == 2:
                load_gate(0, 0.5)
                load_gate(1, 0.5)
                gsl_of = lambda t: cls(t)
            else:
                load_gate(gbase, 0.5)
                gsl_of = (lambda gb: lambda t: gb)(gbase)
            modnorm_to_uT(f)
            ntile_w = 2 * NHC
            nxt = [0]

            def issue_gu(f=f):
                i = nxt[0]
                if i < ntile_w:
                    P.dma("pool", gslot(i % NG), wgu_d[f][i], w=[b_ring[i % NG]])
                    nxt[0] += 1
            nxtd = [0]

            def issue_wd(f=f):
                i = nxtd[0]
                if i < NHC:
                    P.dma("pool", wdr[:, i % NWD, :], wd_d[f][i], w=[b_wd[i % NWD]])
                    nxtd[0] += 1
            for _ in range(NG):
                issue_gu()
            for _ in range(NWD):
                issue_wd()
            nblk = NHC // HBC
            gq = 0

            def down(b, f=f):
                par = b % 2
                for t in range(nt):
                    gi = grp_of_tile(t)
                    for cb in range(4):
                        pb = 4 + (yq[0] % 2)
                        for jj in range(HBC):
                            j = b * HBC + jj
                            P.mm(ps[pb][:, :], hT[:, par, jj, t * 128:(t + 1) * 128],
                                 wdr[:, j % NWD, cb * 512:(cb + 1) * 512], jj == 0, jj == HBC - 1,
                                 r=[b_hT[par][gi], b_wd[j % NWD]], w=[b_ps[pb]])
                        accumulate(t, cb, pb, gsl_of(t))
                for jj in range(HBC):
                    issue_wd()

            for b in range(nblk):
                par = b % 2
                for jj in range(HBC):
                    j = b * HBC + jj
                    sg_, su_ = (2 * j) % NG, (2 * j + 1) % NG
                    for gi, (o, sz) in enumerate(groups):
                        pg, pu = gq % 2, 2 + gq % 2
                        k2 = gq % 2
                        gq += 1
                        for kc in range(KC):
                            P.mm(ps[pg][:, 0:sz], gslot(sg_)[:, kc, :], uT[:, kc, o:o + sz], kc == 0, kc == KC - 1,
                                 r=[b_ring[sg_], b_uT[gi]], w=[b_ps[pg]])
                        for kc in range(KC):
                            P.mm(ps[pu][:, 0:sz], gslot(su_)[:, kc, :], uT[:, kc, o:o + sz], kc == 0, kc == KC - 1,
                                 r=[b_ring[su_], b_uT[gi]], w=[b_ps[pu]])
                        P.act(sgt[:, k2, 0:sz], ps[pg][:, 0:sz], AF.Silu, r=[b_ps[pg]], w=[b_sgt[k2]])
                        P.op("dve", (lambda k2=k2, pu=pu, sz=sz, o=o, par=par, jj=jj: lambda e: e.tensor_tensor(
                            hT[:, par, jj, o:o + sz], sgt[:, k2, 0:sz], ps[pu][:, 0:sz], ALU.mult))(),
                            r=[b_sgt[k2], b_ps[pu]], w=[b_hT[par][gi]])
                    issue_gu()
                    issue_gu()
                if b >= 1:
                    down(b - 1)
            down(nblk - 1)

        if post == "modnorm":
            modnorm_to_uT(nffn)
            P.dma("sp", uT_o, uT[:], r=b_uT, w=[Buf()])
            for t in range(8):
                P.dma("sp", h_o[t * 128:(t + 1) * 128, :], h[:, t, :], r=[b_h[t]], w=[Buf()])
        else:
            load_gate(0, 1.0)
            for t in range(nt):
                c0 = rstd_of(t)
                P.op("dve", (lambda t=t, c0=c0: lambda e: e.scalar_tensor_tensor(
                    h[:, t, :], h[:, t, :], ss[:, c0:c0 + 1], grep[:, 0, :], ALU.mult, ALU.mult))(),
                    r=[b_h[t], b_ss, b_grep[0]], w=[b_h[t]])
                P.dma("sp", y_o[t * 128:(t + 1) * 128, :], h[:, t, :], r=[b_h[t]], w=[Buf()])
        P.emit()
    return nc


def _fm(v):
    return np.ascontiguousarray(v.reshape(KC, 128).T)


def _tile_wgu(w_gu):
    g = w_gu[:, :FH].reshape(KC, 128, NHC, 128)
    u = w_gu[:, FH:].reshape(KC, 128, NHC, 128)
    st = np.stack([g, u], axis=3)
    return np.ascontiguousarray(st.transpose(2, 3, 1, 0, 4).reshape(2 * NHC, 128, KC, 128))


_IDENT = np.eye(128, dtype=np.float32)


def build_fourier():
    nc = bass.Bass("TRN2", target_bir_lowering=False)
    din = lambda name, shape, dt: nc.dram_tensor(name, shape, dt, kind="ExternalInput").ap()
    uT_d = din("uT", [128, 8, 2048], BF16)
    cc_d = din("cc", [128, 4, 512], BF16)
    sc_d = din("sc", [128, 4, 512], BF16)
    ct_d = din("ct", [16, 128, 16, 128], BF16)
    st_d = din("st", [16, 128, 16, 128], BF16)
    yf = nc.dram_tensor("yf", [2048, 1024], BF16, kind="ExternalOutput").ap()
    P = Prog(nc)
    with ExitStack() as es:
        sb = lambda name, shape, dt: es.enter_context(nc.sbuf_tensor(name, shape, dt))
        uT = sb("uT_s", [128, 8, 2048], BF16)
        cc = sb("cc_s", [128, 4, 512], BF16)
        sc = sb("sc_s", [128, 4, 512], BF16)
        pq = sb("pq", [128, 16, 2, 1024], BF16)
        ctr = sb("ctr", [128, 3, 16, 128], BF16)
        strr = sb("strr", [128, 3, 16, 128], BF16)
        yb = sb("yb", [128, 2, 1024], BF16)
        ps = [es.enter_context(nc.psum_tensor(f"ps{i}", [128, 512], F32)) for i in range(8)]
        b_u, b_cc, b_sc = Buf(), Buf(), Buf()
        b_pq = [Buf() for _ in range(16)]
        b_ct = [Buf() for _ in range(3)]
        b_st = [Buf() for _ in range(3)]
        b_yb = [Buf(), Buf()]
        b_ps = [Buf(x=True) for _ in range(8)]
        P.dma("sp", cc[:], cc_d, w=[b_cc])
        P.dma("sp", sc[:], sc_d, w=[b_sc])
        for k in range(8):
            P.dma("sp", uT[:, k, :], uT_d[:, k, :], w=[b_u])
        q = 0
        for tt in range(16):
            for g in range(2):
                for which, (cm, bcm) in enumerate(((cc, b_cc), (sc, b_sc))):
                    pb = q % 4
                    q += 1
                    for c4 in range(4):
                        P.mm(ps[pb][:, :], uT[:, g * 4 + c4, tt * 128:(tt + 1) * 128], cm[:, c4, :], c4 == 0, c4 == 3,
                             r=[b_u, bcm], w=[b_ps[pb]])
                    dst = pq[:, tt, which, g * 512:(g + 1) * 512]
                    if q % 2 == 0:
                        P.act(dst, ps[pb][:, :], AF.Copy, r=[b_ps[pb]], w=[b_pq[tt]])
                    else:
                        P.op("dve", (lambda dst=dst, pb=pb: lambda e: e.tensor_copy(dst, ps[pb][:, :]))(),
                             r=[b_ps[pb]], w=[b_pq[tt]])
        q = 0
        for tp in range(16):
            sl = tp % 3
            P.dma("sp", ctr[:, sl], ct_d[tp], w=[b_ct[sl]])
            P.dma("sp", strr[:, sl], st_d[tp], w=[b_st[sl]])
            for g in range(2):
                pb = 4 + q % 4
                q += 1
                for tc in range(16):
                    P.mm(ps[pb][:, :], ctr[:, sl, tc, :], pq[:, tc, 0, g * 512:(g + 1) * 512], tc == 0, False,
                         r=[b_ct[sl], b_pq[tc]], w=[b_ps[pb]])
                for tc in range(16):
                    P.mm(ps[pb][:, :], strr[:, sl, tc, :], pq[:, tc, 1, g * 512:(g + 1) * 512], False, tc == 15,
                         r=[b_st[sl], b_pq[tc]], w=[b_ps[pb]])
                dst = yb[:, tp % 2, g * 512:(g + 1) * 512]
                if g == 0:
                    P.act(dst, ps[pb][:, :], AF.Copy, r=[b_ps[pb]], w=[b_yb[tp % 2]])
                else:
                    P.op("dve", (lambda dst=dst, pb=pb: lambda e: e.tensor_copy(dst, ps[pb][:, :]))(),
                         r=[b_ps[pb]], w=[b_yb[tp % 2]])
            P.dma("sp", yf[tp * 128:(tp + 1) * 128, :], yb[:, tp % 2, :], r=[b_yb[tp % 2]], w=[Buf()])
        P.emit()
    return nc


def _dft_consts():
    def cs(n):
        k = np.arange(n, dtype=np.int64)
        ph = (np.outer(k, k) % n).astype(np.float64) * (2.0 * np.pi / n)
        return np.cos(ph) / np.sqrt(n), np.sin(ph) / np.sqrt(n)
    cC, sC = cs(512)
    cT, sT = cs(2048)
    fmc = lambda m: np.ascontiguousarray(m.reshape(4, 128, 512).transpose(1, 0, 2)).astype(NPBF)
    tl = lambda m: np.ascontiguousarray(m.reshape(16, 128, 16, 128).transpose(2, 1, 0, 3)).astype(NPBF)
    return dict(cc=fmc(cC), sc=fmc(sC), ct=tl(cT), st=tl(-sT))


NTK = 2304
NCH = 36
TGRP = [(0, 256), (256, 512), (768, 512), (1280, 512), (1792, 512)]
SM_C = 8.0


def build_mixer():
    nc = bass.Bass("TRN2", target_bir_lowering=False)
    din = lambda name, shape, dt: nc.dram_tensor(name, shape, dt, kind="ExternalInput").ap()
    uT_d = din("uT", [128, KC, NTK], BF16)
    wA_d = din("wA", [12, 128, KC, 128], F32)
    wZ_d = din("wZ", [4, 128, KC, 128], F32)
    wG_d = din("wG", [128, KC, 16], F32)
    wQ_d = din("wQ", [128, KC, 512], F32)
    wKV_d = din("wKV", [128, KC, 256], F32)
    cw_d = din("cw", [128, 12, 5], F32)
    sv_d = din("sv", [2, 8], F32)
    nv_d = din("nv", [3, 128], F32)
    rope_d = din("rope", [128, 16, 2, 64], F32)
    msk_d = din("msk", [64, 4, 64], F32)
    id_d = din("ident", [128, 128], F32)
    ocat = nc.dram_tensor("ocat", [2048, 1024], BF16, kind="ExternalOutput").ap()
    P = Prog(nc)
    with ExitStack() as es:
        sb = lambda name, shape, dt: es.enter_context(nc.sbuf_tensor(name, shape, dt))
        uT = sb("uT_s", [128, KC, NTK], BF16)
        wr = sb("wr", [128, 4, KC, 128], BF16)
        wG = sb("wG_s", [128, KC, 16], BF16)
        cw = sb("cw_s", [128, 12, 5], F32)
        ident = sb("ident_s", [128, 128], F32)
        identb = sb("identb", [128, 128], BF16)
        ones = sb("ones_s", [128, 128], F32)
        msk = sb("msk_s", [64, 4, 64], F32)
        svr = sb("svr", [64, 2, 8], F32)
        nvr = sb("nvr", [128, 3, 128], F32)
        cst = sb("cst", [128, 4], F32)
        graw = sb("graw", [64, 16, NCH], F32)
        gl = sb("gl", [64, 8, NCH], F32)
        beta = sb("beta", [64, 8, NCH], F32)
        nbeta = sb("nbeta", [64, 8, NCH], F32)
        gall = sb("gall", [64, 8, NCH], F32)
        eg = sb("eg", [64, 8, NCH], F32)
        ekd = sb("ekd", [64, 8, NCH], F32)
        bg = sb("bg", [64, 8, NCH], F32)
        gtr = sb("gtr", [128, 8, NCH], F32)
        big = sb("big", [128, 5, NTK + 8], F32)
        xp = big[:, 0, :]
        acc = big[:, 1, 0:NTK]
        QnT = big[:, 2, 0:NTK]
        KnT = big[:, 3, 0:NTK]
        vF = big[:, 4, 0:NTK]
        rtmp = sb("rtmp", [128, 512], F32)
        S = sb("S", [128, 128], F32)
        oacc = sb("oacc", [64, 32, 128], F32)
        oout = sb("oout", [64, 2, 128], BF16)
        sm = sb("sm", [64, 16, 128], F32)
        wTs = sb("wTs", [128, 64], F32)
        zs = sb("zs", [64, 128], F32)
        st2 = sb("st2", [128, 16], F32)
        ps = [es.enter_context(nc.psum_tensor(f"ps{i}", [128, 512], F32)) for i in range(7)]
        psb = es.enter_context(nc.psum_tensor("psb", [128, 8, 128], BF16))
        B = lambda n="": Buf(n)
        b_uT, b_wG, b_cw, b_id, b_idb, b_ones, b_msk, b_svr, b_nvr, b_cst = (B() for _ in range(10))
        b_wr = [B() for _ in range(4)]
        b_graw, b_gl, b_beta, b_nbeta, b_gall, b_eg, b_ekd, b_bg, b_gtr = (B() for _ in range(9))
        b_xp, b_acc, b_Q, b_K, b_V, b_rtmp, b_S, b_oacc, b_wTs, b_zs, b_st2 = (B() for _ in range(11))
        b_oout = [B(), B()]
        b_sm = [B() for _ in range(16)]
        b_pp = {}

        bk = [Buf(f"bank{i}", x=True) for i in range(7)]
        BANK = dict(tk=0, tv=0, fm0=0, fm4=0, kk=1, D=1, DT=1, qkT=1, fm1=1, N0=2, Mn=2, Nn=2, Tu=3, u=3, wT=3,
                    vn=4, oA=4, oB=4, Sn=4, g3=5)

        def bp(name):
            return bk[BANK[name]]

        def X(eng, meth, *args, r=(), w=(), **kw):
            P.op(eng, lambda e: getattr(e, meth)(*args, **kw), r, w)
        dve = lambda meth, *args, r=(), w=(), **kw: X("dve", meth, *args, r=r, w=w, **kw)

        for kc in range(0, KC, 4):
            P.dma("sp", uT[:, kc:kc + 4, :], uT_d[:, kc:kc + 4, :], w=[b_uT])
        P.dma("sp", ident[:], id_d, w=[b_id])
        P.dma("sp", msk[:], msk_d, w=[b_msk])
        P.dma("sp", cw[:], cw_d, w=[b_cw])
        P.dma("pool", wG[:], wG_d, w=[b_wG])
        for i in range(2):
            P.dma("sp", svr[:, i, :], sv_d[i:i + 1, :].partition_broadcast(64), w=[b_svr])
        for i in range(3):
            P.dma("sp", nvr[:, i, :], nv_d[i:i + 1, :].partition_broadcast(128), w=[b_nvr])
        X("pool", "memset", ones[:], 1.0, w=[b_ones])
        X("pool", "memset", cst[:, 0:1], EPS, w=[b_cst])
        X("pool", "memset", cst[:, 1:2], 1.0, w=[b_cst])
        X("pool", "memset", cst[:, 2:3], -SM_C, w=[b_cst])
        dve("tensor_copy", identb[:], ident[:], r=[b_id], w=[b_idb])
        P.act(svr[:, 0, :], svr[:, 0, :], AF.Exp, r=[b_svr], w=[b_svr])
        dve("tensor_scalar", svr[:, 0, :], svr[:, 0, :], -1.0, None, ALU.mult, r=[b_svr], w=[b_svr])

        for c in range(NCH):
            reg = ps[5][0:64, (c % 8) * 16:(c % 8) * 16 + 16]
            for kc in range(KC):
                P.mm(reg, uT[:, kc, c * 64:(c + 1) * 64], wG[:, kc, :], kc == 0, kc == KC - 1,
                     r=[b_uT, b_wG], w=[bp("g3")])
            P.act(graw[:, :, c], reg, AF.Copy, r=[bp("g3")], w=[b_graw])
        bc = lambda ap: ap.unsqueeze(2).to_broadcast([64, 8, NCH])
        dve("tensor_tensor", gl[:], graw[:, 0:8, :], bc(svr[:, 1, :]), ALU.add, r=[b_graw, b_svr], w=[b_gl])
        P.act(gl[:], gl[:], AF.Exp, r=[b_gl], w=[b_gl])
        P.act(gl[:], gl[:], AF.Ln, r=[b_gl, b_cst], w=[b_gl], bias=cst[0:64, 1:2])
        dve("tensor_tensor", gl[:], gl[:], bc(svr[:, 0, :]), ALU.mult, r=[b_gl, b_svr], w=[b_gl])
        P.act(beta[:], graw[:, 8:16, :], AF.Sigmoid, r=[b_graw], w=[b_beta])
        dve("tensor_scalar", nbeta[:], beta[:], -1.0, None, ALU.mult, r=[b_beta], w=[b_nbeta])
        fl = lambda ap: ap.rearrange("p a c -> p (a c)")
        for d in range(2):
            P.mm(ps[5][0:64, 128:128 + 4 * NCH], msk[:, 0 if d == 0 else 2, :], fl(gl[:, d * 4:(d + 1) * 4, :]), True, True,
                 r=[b_msk, b_gl], w=[bp("g3")])
            P.act(fl(gall[:, d * 4:(d + 1) * 4, :]), ps[5][0:64, 128:128 + 4 * NCH], AF.Copy, r=[bp("g3")], w=[b_gall])
        P.mm(ps[0][:, 0:8 * NCH], ones[0:64, :], fl(gl[:]), True, True, r=[b_ones, b_gl], w=[bp("fm4")])
        P.act(fl(gtr[:]), ps[0][:, 0:8 * NCH], AF.Exp, r=[bp("fm4")], w=[b_gtr])
        dve("tensor_tensor", fl(ekd[:]), ps[0][0:64, 0:8 * NCH], fl(gall[:]), ALU.subtract, r=[bp("fm4"), b_gall], w=[b_ekd])
        P.act(ekd[:], ekd[:], AF.Exp, r=[b_ekd], w=[b_ekd])
        P.act(eg[:], gall[:], AF.Exp, r=[b_gall], w=[b_eg])
        dve("tensor_tensor", bg[:], beta[:], eg[:], ALU.mult, r=[b_beta, b_eg], w=[b_bg])

        def to_xp(psap, o, sz, bps):
            po = 2 if o == 0 else o + 6
            P.act(xp[:, po:po + sz], psap, AF.Copy, r=[bps], w=[b_xp])

        def fm_project(slot):
            for gi, (o, sz) in enumerate(TGRP):
                pb = gi % 2
                for kc in range(KC):
                    P.mm(ps[pb][:, 0:sz], wr[:, slot, kc, :], uT[:, kc, o:o + sz], kc == 0, kc == KC - 1,
                         r=[b_wr[slot], b_uT], w=[bp(f"fm{pb}")])
                to_xp(ps[pb][:, 0:sz], o, sz, bp(f"fm{pb}"))

        def conv_silu(ci, dst, b_dst):
            for (oo, po, ln) in [(0, 0, 256), (256, 260, 2048)]:
                for j in range(5):
                    if j == 0:
                        dve("tensor_scalar", acc[:, oo:oo + ln], xp[:, po + j:po + j + ln], cw[:, ci, j:j + 1], None, ALU.mult,
                            r=[b_xp, b_cw], w=[b_acc])
                    else:
                        dve("scalar_tensor_tensor", acc[:, oo:oo + ln], xp[:, po + j:po + j + ln], cw[:, ci, j:j + 1],
                            acc[:, oo:oo + ln], ALU.mult, ALU.add, r=[b_xp, b_cw, b_acc], w=[b_acc])
            P.act(dst, acc, AF.Silu, r=[b_acc], w=[b_dst])

        def l2norm_fm(dst, b_dst, scale):
            P.act(xp[:, 0:NTK], dst, AF.Square, r=[b_dst], w=[b_xp])
            for gi, (o, sz) in enumerate(TGRP):
                pb = gi % 2
                P.mm(ps[pb][:, 0:sz], ones[:], xp[:, o:o + sz], True, True, r=[b_ones, b_xp], w=[bp(f"fm{pb}")])
                P.act(rtmp[:, 0:sz], ps[pb][:, 0:sz], AF.Sqrt, r=[bp(f"fm{pb}"), b_cst], w=[b_rtmp], bias=cst[:, 0:1])
                dve("reciprocal", rtmp[:, 0:sz], rtmp[:, 0:sz], r=[b_rtmp], w=[b_rtmp])
                dve("scalar_tensor_tensor", dst[:, o:o + sz], dst[:, o:o + sz], scale, rtmp[:, 0:sz], ALU.mult, ALU.mult,
                    r=[b_dst, b_rtmp], w=[b_dst])

        smt = lambda i: sm[:, i, 0:64]
        I64 = ident[0:64, 0:64]
        for h in range(4):
            for qi in range(3):
                P.dma("pool", wr[:, qi], wA_d[qi * 4 + h], w=[b_wr[qi]])
            P.dma("pool", wr[:, 3], wZ_d[h], w=[b_wr[3]])
            X("pool", "memset", xp, 0.0, r=[b_xp], w=[b_xp])
            for qi, (dst, b_dst) in enumerate(((QnT, b_Q), (KnT, b_K), (vF, b_V))):
                fm_project(qi)
                conv_silu(qi * 4 + h, dst, b_dst)
                if qi < 2:
                    l2norm_fm(dst, b_dst, (128.0 ** -0.5) if qi == 0 else 1.0)
                    X("pool", "memset", xp, 0.0, r=[b_xp], w=[b_xp])
            for d in range(2):
                col = d * 4 + h
                mTRI, mSM = (0, 1) if d == 0 else (2, 3)
                X("pool", "memset", S[:], 0.0, r=[b_S], w=[b_S])
                order = list(range(NCH)) if d == 0 else [3, 2, 1, 0] + list(range(NCH - 1, 3, -1))
                for c in order:
                    tok = slice(c * 64, (c + 1) * 64)
                    cs = slice(c, c + 1)
                    P.tr(ps[0][0:64, 0:128], KnT[:, tok], ident[:], r=[b_K, b_id], w=[bp("tk")])
                    P.tr(ps[0][0:64, 128:256], vF[:, tok], ident[:], r=[b_V, b_id], w=[bp("tv")])
                    P.act(sm[:, 0, :], ps[0][0:64, 0:128], AF.Copy, r=[bp("tk"), b_ekd], w=[b_sm[0]], scale=ekd[:, col, cs])
                    dve("tensor_scalar", sm[:, 1, :], ps[0][0:64, 0:128], bg[:, col, cs], None, ALU.mult,
                        r=[bp("tk"), b_bg], w=[b_sm[1]])
                    P.act(sm[:, 2, :], ps[0][0:64, 128:256], AF.Copy, r=[bp("tv"), b_beta], w=[b_sm[2]], scale=beta[:, col, cs])
                    dve("tensor_scalar", smt(15), msk[:, mSM, :], gl[:, col, cs], None, ALU.mult,
                        r=[b_msk, b_gl], w=[b_sm[15]])
                    P.mm(ps[1][0:64, 0:64], KnT[:, tok], KnT[:, tok], True, True, r=[b_K], w=[bp("kk")])
                    P.mm(ps[1][0:64, 64:128], msk[:, mTRI, :], smt(15), True, True, r=[b_msk, b_sm[15]], w=[bp("D")])
                    P.mm(ps[1][0:64, 128:192], smt(15), msk[:, mTRI, :], True, True, r=[b_msk, b_sm[15]], w=[bp("DT")])
                    P.mm(ps[1][0:64, 192:256], KnT[:, tok], QnT[:, tok], True, True, r=[b_K, b_Q], w=[bp("qkT")])
                    P.act(smt(3), ps[1][0:64, 64:128], AF.Exp, r=[bp("D")], w=[b_sm[3]])
                    P.act(smt(4), ps[1][0:64, 128:192], AF.Exp, r=[bp("DT")], w=[b_sm[4]])
                    dve("tensor_tensor", smt(3), smt(3), msk[:, mSM, :], ALU.mult, r=[b_sm[3], b_msk], w=[b_sm[3]])
                    dve("tensor_tensor", smt(4), smt(4), msk[:, mTRI, :], ALU.mult, r=[b_sm[4], b_msk], w=[b_sm[4]])
                    dve("scalar_tensor_tensor", smt(5), ps[1][0:64, 0:64], nbeta[:, col, cs], smt(3), ALU.mult, ALU.mult,
                        r=[bp("kk"), b_nbeta, b_sm[3]], w=[b_sm[5]])
                    dve("tensor_tensor", smt(11), ps[1][0:64, 192:256], smt(4), ALU.mult, r=[bp("qkT"), b_sm[4]], w=[b_sm[11]])
                    P.tr(ps[2][0:64, 192:256], smt(5), I64, r=[b_sm[5], b_id], w=[bp("N0")])
                    P.act(smt(7), ps[2][0:64, 192:256], AF.Copy, r=[bp("N0")], w=[b_sm[7]])
                    dve("tensor_tensor", smt(9), ps[2][0:64, 192:256], I64, ALU.add, r=[bp("N0"), b_id], w=[b_sm[9]])
                    mi, ni, ti = 5, 7, 9
                    for lv in range(5):
                        mo_, no_, to_ = (6 if mi == 5 else 5), (8 if ni == 7 else 7), (10 if ti == 9 else 9)
                        P.mm(ps[2][0:64, 0:64], smt(ni), smt(mi), True, True, r=[b_sm[ni], b_sm[mi]], w=[bp("Mn")])
                        if lv < 4:
                            P.mm(ps[2][0:64, 64:128], smt(mi), smt(ni), True, True, r=[b_sm[ni], b_sm[mi]], w=[bp("Nn")])
                        P.act(smt(mo_), ps[2][0:64, 0:64], AF.Copy, r=[bp("Mn")], w=[b_sm[mo_]])
                        if lv < 4:
                            dve("tensor_copy", smt(no_), ps[2][0:64, 64:128], r=[bp("Nn")], w=[b_sm[no_]])
                        P.mm(ps[3][0:64, 0:64], smt(mo_), smt(ti), True, True, r=[b_sm[mo_], b_sm[ti]], w=[bp("Tu")])
                        dve("tensor_tensor", smt(to_), ps[3][0:64, 0:64], smt(ti), ALU.add,
                            r=[bp("Tu"), b_sm[ti]], w=[b_sm[to_]])
                        mi, ni, ti = mo_, no_, to_
                    P.mm(ps[3][0:64, 128:256], smt(ti), sm[:, 2, :], True, True, r=[b_sm[ti], b_sm[2]], w=[bp("u")])
                    P.mm(ps[3][:, 256:320], sm[:, 1, :], smt(ti), True, True, r=[b_sm[ti], b_sm[1]], w=[bp("wT")])
                    P.act(sm[:, 12, :], ps[3][0:64, 128:256], AF.Copy, r=[bp("u")], w=[b_sm[12]])
                    dve("tensor_copy", wTs[:], ps[3][:, 256:320], r=[bp("wT")], w=[b_wTs])
                    P.mm(ps[4][0:64, 0:128], wTs[:], S[:], True, True, r=[b_wTs, b_S], w=[bp("vn")])
                    dve("tensor_tensor", sm[:, 13, :], sm[:, 12, :], ps[4][0:64, 0:128], ALU.subtract,
                        r=[b_sm[12], bp("vn")], w=[b_sm[13]])
                    if c >= 4:
                        P.mm(ps[4][0:64, 128:256], QnT[:, tok], S[:], True, True, r=[b_Q, b_S], w=[bp("oA")])
                        P.mm(ps[4][0:64, 256:384], smt(11), sm[:, 13, :], True, True, r=[b_sm[11], b_sm[13]], w=[bp("oB")])
                        P.act(sm[:, 14, :], ps[4][0:64, 256:384], AF.Copy, r=[bp("oB")], w=[b_sm[14]])
                        if d == 0:
                            dve("scalar_tensor_tensor", oacc[:, c - 4, :], ps[4][0:64, 128:256], eg[:, col, cs], sm[:, 14, :],
                                ALU.mult, ALU.add, r=[bp("oA"), b_eg, b_sm[14]], w=[b_oacc])
                        else:
                            dve("scalar_tensor_tensor", sm[:, 14, :], ps[4][0:64, 128:256], eg[:, col, cs], sm[:, 14, :],
                                ALU.mult, ALU.add, r=[bp("oA"), b_eg, b_sm[14]], w=[b_sm[14]])
                            X("pool", "tensor_tensor", oacc[:, c - 4, :], oacc[:, c - 4, :], sm[:, 14, :], ALU.add,
                              r=[b_sm[14], b_oacc], w=[b_oacc])
                    P.mm(ps[4][:, 384:512], sm[:, 0, :], sm[:, 13, :], True, True, r=[b_sm[0], b_sm[13]], w=[bp("Sn")])
                    dve("scalar_tensor_tensor", S[:], S[:], gtr[:, col, cs], ps[4][:, 384:512], ALU.mult, ALU.add,
                        r=[b_S, b_gtr, bp("Sn")], w=[b_S])
            for c in range(4, NCH):
                for kc in range(KC):
                    P.mm(ps[5][0:64, 0:128], uT[:, kc, c * 64:(c + 1) * 64], wr[:, 3, kc, :], kc == 0, kc == KC - 1,
                         r=[b_uT, b_wr[3]], w=[bp("g3")])
                P.act(zs[:], ps[5][0:64, 0:128], AF.Silu, r=[bp("g3")], w=[b_zs])
                P.act(sm[:, 12, :], oacc[:, c - 4, :], AF.Square, r=[b_oacc], w=[b_sm[12], b_st2],
                      accum_out=st2[0:64, 0:1])
                P.act(st2[0:64, 1:2], st2[0:64, 0:1], AF.Sqrt, r=[b_st2, b_cst], w=[b_st2], scale=1.0 / 128,
                      bias=cst[0:64, 0:1])
                dve("reciprocal", st2[0:64, 0:1], st2[0:64, 1:2], r=[b_st2], w=[b_st2])
                dve("scalar_tensor_tensor", sm[:, 12, :], oacc[:, c - 4, :], st2[0:64, 0:1], nvr[0:64, 0, :], ALU.mult, ALU.mult,
                    r=[b_oacc, b_st2, b_nvr], w=[b_sm[12]])
                dve("tensor_tensor", oout[:, c % 2, :], sm[:, 12, :], zs[:], ALU.mult, r=[b_sm[12], b_zs], w=[b_oout[c % 2]])
                P.dma("sp", ocat[(c - 4) * 64:(c - 3) * 64, h * 128:(h + 1) * 128], oout[:, c % 2, :],
                      r=[b_oout[c % 2]], w=[Buf()])

        b_psb = Buf('psb', x=True)
        wKV = sb("wKV_s", [128, KC, 256], BF16)
        wQ = wr[:].rearrange("p a k n -> p (a k n)").rearrange("p (k n) -> p k n", k=KC)
        f01 = big[:, 0:2, :].rearrange("p a n -> p (a n)").bitcast(BF16)
        QT = f01[:, 0:8192].rearrange("p (h n) -> p h n", h=4)
        KT = big[:, 2, :].bitcast(BF16)[:, 0:NTK]
        V = big[:, 3, :].bitcast(BF16)[:, 0:18 * 132].rearrange("p (t f) -> p t f", t=18)
        rope = big[:, 4, 0:2048].rearrange("p (t c f) -> p t c f", t=16, c=2)
        qf = sb("qf", [128, 5, 128], F32)
        qr = sb("qr", [128, 5, 128], BF16)
        t1 = sb("t1", [128, 5, 2, 32], F32)
        t2 = sb("t2", [128, 5, 2, 32], F32)
        sta = sb("sta", [128, 16], F32)
        PT = sb("PT", [128, 2, 512], BF16)
        oatt = sb("oatt", [128, 2, 4, 128], BF16)
        rs = sb("rs", [128, 4], F32)
        b_wQ, b_wKV, b_rope, b_QT, b_KT, b_V2, b_qf, b_qr, b_t1, b_t2, b_sta, b_rs = (B() for _ in range(12))
        b_PT = [B(), B()]
        b_oatt = [B(), B()]
        X("pool", "memset", st2[:, 8:9], 0.0, w=bk + b_wr + [
            b_psb, b_st2, b_xp, b_acc, b_Q, b_K, b_V, b_wQ, b_rope, b_QT, b_KT, b_V2])
        P.dma("pool", wQ, wQ_d, w=[b_wQ])
        P.dma("pool", wKV[:], wKV_d, w=[b_wKV])
        P.dma("sp", rope, rope_d, w=[b_rope])
        X("pool", "memset", V, 1.0, w=[b_V2])
        for tt in range(18):
            lat = tt >= 2
            tok = slice(tt * 128, (tt + 1) * 128)
            nh = 5 if lat else 1
            s0 = 0 if lat else 4
            if lat:
                for kc in range(KC):
                    P.mm(ps[6][:, :], uT[:, kc, tok], wQ[:, kc, :], kc == 0, kc == KC - 1, r=[b_uT, b_wQ], w=[bk[6]])
            for kc in range(KC):
                P.mm(ps[5][:, 0:256], uT[:, kc, tok], wKV[:, kc, :], kc == 0, kc == KC - 1, r=[b_uT, b_wKV], w=[bk[5]])
            P.act(V[:, tt, 0:128], ps[5][:, 128:256], AF.Copy, r=[bk[5]], w=[b_V2])
            srcs = ([(ps[6][:, hh * 128:(hh + 1) * 128], bk[6], 1, hh) for hh in range(4)] if lat else []) + \
                   [(ps[5][:, 0:128], bk[5], 2, 4)]
            for (src, bsrc, gi, slot) in srcs:
                P.act(qf[:, slot, :], src, AF.Square, r=[bsrc], w=[b_qf, b_sta], accum_out=sta[:, slot:slot + 1])
            P.act(sta[:, 8 + s0:13], sta[:, s0:5], AF.Sqrt, r=[b_sta, b_cst], w=[b_sta], scale=1.0 / 128, bias=cst[:, 0:1])
            dve("reciprocal", sta[:, s0:5], sta[:, 8 + s0:13], r=[b_sta], w=[b_sta])
            for (src, bsrc, gi, slot) in srcs:
                dve("scalar_tensor_tensor", qf[:, slot, :], src, sta[:, slot:slot + 1], nvr[:, gi, :], ALU.mult, ALU.mult,
                    r=[bsrc, b_sta, b_nvr], w=[b_qf])
            if lat:
                ti_ = tt - 2
                xv = qf[:, :, :].rearrange("p h (a t f) -> p h a t f", a=2, t=2)
                ov = qr[:, :, :].rearrange("p h (a t f) -> p h a t f", a=2, t=2)
                cosb = rope[:, ti_, 0, :].rearrange("p (a f) -> p a f", a=2).unsqueeze(1).to_broadcast([128, 5, 2, 32])
                sinb = rope[:, ti_, 1, :].rearrange("p (a f) -> p a f", a=2).unsqueeze(1).to_broadcast([128, 5, 2, 32])
                X1, X2 = xv[:, :, :, 0, :], xv[:, :, :, 1, :]
                dve("tensor_tensor", t1[:], X1, cosb, ALU.mult, r=[b_qf, b_rope], w=[b_t1])
                dve("tensor_tensor", t2[:], X2, sinb, ALU.mult, r=[b_qf, b_rope], w=[b_t2])
                dve("tensor_tensor", ov[:, :, :, 0, :], t1[:], t2[:], ALU.subtract, r=[b_t1, b_t2], w=[b_qr])
                dve("tensor_tensor", t1[:], X1, sinb, ALU.mult, r=[b_qf, b_rope], w=[b_t1])
                dve("tensor_tensor", t2[:], X2, cosb, ALU.mult, r=[b_qf, b_rope], w=[b_t2])
                dve("tensor_tensor", ov[:, :, :, 1, :], t1[:], t2[:], ALU.add, r=[b_t1, b_t2], w=[b_qr])
            else:
                dve("tensor_copy", qr[:, 4, :], qf[:, 4, :], r=[b_qf], w=[b_qr])
            for slot in range(s0, 5):
                P.tr(psb[:, slot, :], qr[:, slot, :], identb[:], r=[b_qr, b_idb], w=[b_psb])
            if lat:
                P.act(QT[:, :, (tt - 2) * 128:(tt - 1) * 128], psb[:, 0:4, :], AF.Copy, r=[b_psb], w=[b_QT])
            dve("tensor_copy", KT[:, tok], psb[:, 4, :], r=[b_psb], w=[b_KT])
        scl = 128.0 ** -0.5
        q = 0
        aob = [2, 3, 4, 6]
        for hh in range(4):
            for qg in range(4):
                for kt in range(18):
                    k2 = q % 2
                    q += 1
                    P.mm(ps[k2][:, :], KT[:, kt * 128:(kt + 1) * 128], QT[:, hh, qg * 512:(qg + 1) * 512], True, True,
                         r=[b_KT, b_QT], w=[bk[k2]])
                    P.act(PT[:, k2, :], ps[k2][:, :], AF.Exp, r=[bk[k2], b_cst], w=[b_PT[k2]], scale=scl, bias=cst[:, 2:3])
                    for j in range(4):
                        P.mm(ps[aob[j]][:, 0:129], PT[:, k2, j * 128:(j + 1) * 128], V[:, kt, 0:129], kt == 0, kt == 17,
                             r=[b_PT[k2], b_V2], w=[bk[aob[j]]])
                par = (hh * 4 + qg) % 2
                for j in range(4):
                    dve("reciprocal", rs[:, j:j + 1], ps[aob[j]][:, 128:129], r=[bk[aob[j]]], w=[b_rs])
                    dve("tensor_scalar", oatt[:, par, j, :], ps[aob[j]][:, 0:128], rs[:, j:j + 1], None,
                        ALU.mult, r=[bk[aob[j]], b_rs], w=[b_oatt[par]])
                P.dma("sp", ocat[qg * 512:(qg + 1) * 512, 512 + hh * 128:512 + (hh + 1) * 128].rearrange(
                    "(j p) f -> p j f", p=128), oatt[:, par], r=[b_oatt[par]], w=[Buf()])
        P.emit()
    return nc


def _mixer_consts():
    i = np.arange(64)
    le = (i[:, None] <= i[None, :]).astype(np.float32)
    gt = (i[:, None] > i[None, :]).astype(np.float32)
    ge = (i[:, None] >= i[None, :]).astype(np.float32)
    lt = (i[:, None] < i[None, :]).astype(np.float32)
    msk = np.ascontiguousarray(np.stack([le, gt, ge, lt], axis=1))
    t = np.arange(2048)
    pos = np.stack([t // 64, t % 64], axis=1).astype(np.float64)
    inv = 10000.0 ** (-np.arange(32, dtype=np.float64) / 32)
    ang = pos[:, :, None] * inv[None, None, :]
    tab = np.stack([np.cos(ang), np.sin(ang)], axis=1).reshape(2048, 2, 64)
    rope = np.ascontiguousarray(tab.reshape(16, 128, 2, 64).transpose(1, 0, 2, 3)).astype(np.float32)
    return dict(msk=msk, rope=rope, ident=_IDENT)


def _tile_k(w):
    return np.ascontiguousarray(w.reshape(KC, 128, -1).transpose(1, 0, 2))


def _mixer_weights(w_in, conv_w, a_log, dt_bias, gdn_norm, q_norm, k_norm, hh):
    hs = [4 * hh + h for h in range(4)]
    wA = np.stack([_tile_k(w_in[:, qi * 1024 + g * 128: qi * 1024 + (g + 1) * 128]) for qi in range(3) for g in hs])
    wZ = np.stack([_tile_k(w_in[:, 3072 + g * 128: 3072 + (g + 1) * 128]) for g in hs])
    gcols = [4096 + gg * 8 + g for gg in range(4) for g in hs]
    wG = _tile_k(w_in[:, gcols])
    wQ = _tile_k(w_in[:, 4128 + hh * 512: 4128 + (hh + 1) * 512])
    wKV = _tile_k(np.concatenate([w_in[:, 5152 + hh * 128: 5152 + (hh + 1) * 128],
                                  w_in[:, 5408 + hh * 128: 5408 + (hh + 1) * 128]], axis=1))
    cw = np.stack([conv_w[:, qi * 1024 + g * 128: qi * 1024 + (g + 1) * 128].T for qi in range(3) for g in hs], axis=1)
    sv = np.stack([np.concatenate([a_log[0, hs], a_log[1, hs]]), np.concatenate([dt_bias[0, hs], dt_bias[1, hs]])])
    nv = np.stack([gdn_norm, q_norm, k_norm])
    f = lambda a: np.ascontiguousarray(a, dtype=np.float32)
    return dict(wA=f(wA), wZ=f(wZ), wG=f(wG), wQ=f(wQ), wKV=f(wKV), cw=f(cw), sv=f(sv), nv=f(nv))


import os


def _dbg(name, arr):
    d = os.environ.get("MK_DEBUG_DIR")
    if d:
        np.save(os.path.join(d, name + ".npy"), np.asarray(arr))


def _vT(vecs):
    return np.ascontiguousarray(np.stack([_fm(np.asarray(v, np.float32)) for v in vecs], axis=1))


def _tile_proj(w):
    return np.ascontiguousarray(w.reshape(KC, 128, 4, 512).transpose(2, 1, 0, 3))


def _aT(feat):
    return np.ascontiguousarray(feat.reshape(1024, KC, 128).transpose(2, 1, 0))


def kernel(x, c, ctx, c_ctx, w_mod, b_mod, norm_w, ffn_w_gu, ffn_w_down, w_in, conv_w, a_log, dt_bias,
           gdn_norm, q_norm, k_norm, w_out, w_fourier, final_norm):
    f32 = lambda a: np.asarray(a, dtype=np.float32)
    x, c, ctx, c_ctx, w_mod, b_mod, norm_w = map(f32, (x, c, ctx, c_ctx, w_mod, b_mod, norm_w))
    ffn_w_gu, ffn_w_down, w_in, conv_w, a_log, dt_bias = map(f32, (ffn_w_gu, ffn_w_down, w_in, conv_w, a_log, dt_bias))
    gdn_norm, q_norm, k_norm, w_out, w_fourier, final_norm = map(f32, (gdn_norm, q_norm, k_norm, w_out, w_fourier, final_norm))
    m = run_mod(c, c_ctx, w_mod, b_mod)
    _dbg("m", m)
    ctx_flat = ctx.reshape(1024, D)
    wd_t = lambda l, s: np.ascontiguousarray(ffn_w_down[l, s].reshape(NHC, 128, D))

    wgu = _tile_wgu(ffn_w_gu[0, 0])
    wd = wd_t(0, 0)
    in_maps = []
    for cid in range(NCORES):
        b, half = divmod(cid, 2)
        hin = np.concatenate([x[b, half * 1024:(half + 1) * 1024], ctx_flat[cid * 128:(cid + 1) * 128]], axis=0)
        vT = _vT([norm_w[0, 0], m[0, b, 0], m[0, b, 1], norm_w[0, 0], m[0, 4, 0], m[0, 4, 1],
                  norm_w[0, 1], m[0, b, 3], m[0, b, 4], norm_w[0, 1], m[0, 4, 3], m[0, 4, 4]])
        vrow = np.ascontiguousarray(np.stack([m[0, b, 2], m[0, 4, 2]]))
        in_maps.append(dict(hin=np.ascontiguousarray(hin), vT=vT, vrow=vrow, ident=_IDENT, wgu0=wgu, wd0=wd))
    r1 = _run(build_stage(9, False, 1, "modnorm", True), in_maps)
    h1 = [np.asarray(r["h_o"]) for r in r1]
    u1 = [np.asarray(r["uT_o"]) for r in r1]
    _dbg("h1_0", h1[0])
    _dbg("u1_0", u1[0].astype(np.float32))

    consts = _mixer_consts()
    mw = [_mixer_weights(w_in[0], conv_w[0], a_log[0], dt_bias[0], gdn_norm[0], q_norm[0], k_norm[0], hh) for hh in range(2)]
    in_maps = []
    for cid in range(NCORES):
        b, hh = divmod(cid, 2)
        uT = np.concatenate([u1[2 * b][:, :, 1024:1152], u1[2 * b + 1][:, :, 1024:1152],
                             u1[2 * b][:, :, 0:1024], u1[2 * b + 1][:, :, 0:1024]], axis=2)
        in_maps.append(dict(uT=np.ascontiguousarray(uT), **mw[hh], **consts))
    r2 = _run(build_mixer(), in_maps)
    oc = [np.asarray(r["ocat"]) for r in r2]
    _dbg("ocat_0", oc[0].astype(np.float32))

    wgu0, wd0 = _tile_wgu(ffn_w_gu[0, 1]), wd_t(0, 1)
    wgu1, wd1 = _tile_wgu(ffn_w_gu[1, 0]), wd_t(1, 0)
    wp = _tile_proj(w_out[0])
    in_maps = []
    for cid in range(NCORES):
        b, half = divmod(cid, 2)
        o0, o1 = oc[2 * b], oc[2 * b + 1]
        feat = np.concatenate([o0[:, :512], o1[:, :512], o0[:, 512:], o1[:, 512:]], axis=1)[half * 1024:(half + 1) * 1024]
        vT = _vT([norm_w[0, 2], m[0, b, 6], m[0, b, 7], norm_w[1, 0], m[1, b, 0], m[1, b, 1],
                  norm_w[1, 1], m[1, b, 3], m[1, b, 4]])
        vrow = np.ascontiguousarray(np.stack([m[0, b, 5], m[0, b, 8], m[1, b, 2]]))
        in_maps.append(dict(hin=h1[cid], vT=vT, vrow=vrow, ident=_IDENT, wgu0=wgu0, wd0=wd0, wgu1=wgu1, wd1=wd1,
                            aT=_aT(feat), wp=wp))
    r3 = _run(build_stage(8, True, 2, "modnorm", False), in_maps)
    h3 = [np.asarray(r["h_o"]) for r in r3]
    u3 = [np.asarray(r["uT_o"]) for r in r3]
    _dbg("h3_0", h3[0])
    _dbg("u3_0", u3[0].astype(np.float32))

    dc = _dft_consts()
    in_maps = []
    for cid in range(NCORES):
        b, gp = divmod(cid, 2)
        uT = np.concatenate([u3[2 * b][:, gp * 8:(gp + 1) * 8, :], u3[2 * b + 1][:, gp * 8:(gp + 1) * 8, :]], axis=2)
        in_maps.append(dict(uT=np.ascontiguousarray(uT), **dc))
    r4 = _run(build_fourier(), in_maps)
    yf = [np.asarray(r["yf"]) for r in r4]
    _dbg("yf_0", yf[0].astype(np.float32))

    wgu0, wd0 = _tile_wgu(ffn_w_gu[1, 1]), wd_t(1, 1)
    wp = _tile_proj(w_fourier[0])
    in_maps = []
    for cid in range(NCORES):
        b, half = divmod(cid, 2)
        feat = np.concatenate([yf[2 * b], yf[2 * b + 1]], axis=1)[half * 1024:(half + 1) * 1024]
        vT = _vT([norm_w[1, 2], m[1, b, 6], m[1, b, 7]])
        vrow = np.ascontiguousarray(np.stack([m[1, b, 5], m[1, b, 8], final_norm]))
        in_maps.append(dict(hin=h3[cid], vT=vT, vrow=vrow, ident=_IDENT, wgu0=wgu0, wd0=wd0, aT=_aT(feat), wp=wp))
    r5 = _run(build_stage(8, True, 1, "final", False), in_maps)
    out = np.stack([np.concatenate([np.asarray(r5[2 * b]["y_o"]), np.asarray(r5[2 * b + 1]["y_o"])], axis=0) for b in range(4)])
    return out.astype(np.float32)
```
